# Optimizing a Trainium2 kernel written in Bass

```python
import jax, jax.numpy as jnp
from jax import lax
import numpy as np

D_MODEL = 2048
BATCH = 4
SEQ = 4096
DEPTH = 4

N_MIXERS = 3
EXPAND = 2
D_INNER = EXPAND * D_MODEL
CHUNK = 128
A_GROUPS = 8
A_GROUP_DIM = D_INNER // A_GROUPS
B_GROUPS = 8
B_GROUP_DIM = D_INNER // B_GROUPS
POOL_WINDOWS = (2, 4, 8, 16)
C_GROUPS = len(POOL_WINDOWS)
C_GROUP_DIM = D_INNER // C_GROUPS
EPS = 1e-6

kernel_name = "hybrid_gmlp_fnet_pool_encoder"


def _layers_of_kind(kind):
    return len(range(kind, DEPTH, N_MIXERS))


def rmsnorm(x, g):
    xf = x.astype(jnp.float32)
    y = xf * lax.rsqrt(jnp.mean(xf * xf, axis=-1, keepdims=True) + EPS)
    return (y * g.astype(jnp.float32)).astype(x.dtype)


def gmlp_mixer(h, w_in, v_gain, w_s, b_s):
    bsz, seq, _ = h.shape
    z = h @ w_in
    u, v, gate = jnp.split(z, 3, axis=-1)
    u = jax.nn.gelu(u)
    v = rmsnorm(jax.nn.gelu(v), v_gain)
    vc = v.reshape(bsz, seq // CHUNK, CHUNK, A_GROUPS, A_GROUP_DIM)
    sv = jnp.einsum('gpq,bnqgc->bnpgc', w_s, vc) + b_s.T[None, None, :, :, None]
    y = u * sv.reshape(bsz, seq, D_INNER)
    return y, gate


def fourier_mixer(h, w_in, w_mix):
    bsz, seq, _ = h.shape
    z = h @ w_in
    xb, gate = jnp.split(z, 2, axis=-1)
    xg = xb.reshape(bsz, seq, B_GROUPS, B_GROUP_DIM).astype(jnp.float32)
    f = jnp.fft.fft2(xg, axes=(1, 3), norm="ortho").real.astype(xb.dtype)
    y = jnp.einsum('bsgc,gcd->bsgd', f, w_mix).reshape(bsz, seq, D_INNER)
    return y, gate


def pool_mixer(h, w_in, w_mix, scale):
    bsz, seq, _ = h.shape
    z = h @ w_in
    xc, gate = jnp.split(z, 2, axis=-1)
    xg = xc.reshape(bsz, seq, C_GROUPS, C_GROUP_DIM)
    t = jnp.arange(seq)
    pooled = []
    for gi, w in enumerate(POOL_WINDOWS):
        xi = xg[:, :, gi, :].astype(jnp.float32)
        csum = jnp.pad(lax.cumsum(xi, axis=1), ((0, 0), (1, 0), (0, 0)))
        lo = jnp.clip(t - w // 2, 0, seq - 1)
        hi = jnp.clip(t + w - 1 - w // 2, 0, seq - 1)
        wsum = jnp.take(csum, hi + 1, axis=1) - jnp.take(csum, lo, axis=1)
        cnt = (hi - lo + 1).astype(jnp.float32)[None, :, None]
        pooled.append(wsum / cnt - xi)
    p = jnp.stack(pooled, axis=2).astype(xc.dtype)
    y = jnp.einsum('bsgc,gcd->bsgd', p, w_mix).reshape(bsz, seq, D_INNER) * scale
    return y, gate


def setup_inputs(seed: int = 0) -> dict:
    key = jax.random.key(seed)
    ks = jax.random.split(key, 20)
    na, nb, nc = _layers_of_kind(0), _layers_of_kind(1), _layers_of_kind(2)
    f32 = jnp.float32
    nrm = lambda k, shape, s: jax.random.normal(k, shape, f32) * s
    din = D_MODEL ** -0.5
    dout = D_INNER ** -0.5
    return {
        "x": jax.random.normal(ks[0], (BATCH, SEQ, D_MODEL), f32),
        "a_norm": 1.0 + nrm(ks[1], (na, D_MODEL), 0.05),
        "a_w_in": nrm(ks[2], (na, D_MODEL, 3 * D_INNER), din),
        "a_v_gain": 1.0 + nrm(ks[3], (na, D_INNER), 0.05),
        "a_w_s": nrm(ks[4], (na, A_GROUPS, CHUNK, CHUNK), 0.5 * CHUNK ** -0.5),
        "a_b_s": 1.0 + nrm(ks[5], (na, A_GROUPS, CHUNK), 0.1),
        "a_w_out": nrm(ks[6], (na, D_INNER, D_MODEL), dout),
        "b_norm": 1.0 + nrm(ks[7], (nb, D_MODEL), 0.05),
        "b_w_in": nrm(ks[8], (nb, D_MODEL, 2 * D_INNER), din),
        "b_w_mix": nrm(ks[9], (nb, B_GROUPS, B_GROUP_DIM, B_GROUP_DIM), B_GROUP_DIM ** -0.5),
        "b_w_out": nrm(ks[10], (nb, D_INNER, D_MODEL), dout),
        "c_norm": 1.0 + nrm(ks[11], (nc, D_MODEL), 0.05),
        "c_w_in": nrm(ks[12], (nc, D_MODEL, 2 * D_INNER), din),
        "c_w_mix": nrm(ks[13], (nc, C_GROUPS, C_GROUP_DIM, C_GROUP_DIM), C_GROUP_DIM ** -0.5),
        "c_scale": 1.0 + nrm(ks[14], (nc, D_INNER), 0.1),
        "c_w_out": nrm(ks[15], (nc, D_INNER, D_MODEL), dout),
        "final_norm": 1.0 + nrm(ks[16], (D_MODEL,), 0.05),
    }


def reference(x, a_norm, a_w_in, a_v_gain, a_w_s, a_b_s, a_w_out,
              b_norm, b_w_in, b_w_mix, b_w_out,
              c_norm, c_w_in, c_w_mix, c_scale, c_w_out, final_norm):
    for i in range(DEPTH):
        kind, j = i % N_MIXERS, i // N_MIXERS
        if kind == 0:
            h = rmsnorm(x, a_norm[j])
            y, gate = gmlp_mixer(h, a_w_in[j], a_v_gain[j], a_w_s[j], a_b_s[j])
            w_out = a_w_out[j]
        elif kind == 1:
            h = rmsnorm(x, b_norm[j])
            y, gate = fourier_mixer(h, b_w_in[j], b_w_mix[j])
            w_out = b_w_out[j]
        else:
            h = rmsnorm(x, c_norm[j])
            y, gate = pool_mixer(h, c_w_in[j], c_w_mix[j], c_scale[j])
            w_out = c_w_out[j]
        x = x + (y * jax.nn.silu(gate)) @ w_out
    return rmsnorm(x, final_norm)
```

```python
import numpy as np
import ml_dtypes
import concourse.bass as bass
import concourse.mybir as mybir
from concourse.bass_utils import run_bass_kernel_spmd

F32 = mybir.dt.float32
BF16 = mybir.dt.bfloat16
AF = mybir.ActivationFunctionType
ALU = mybir.AluOpType
AX = mybir.AxisListType

D = 2048
E = 4096
KD = D // 128
KE = E // 128
TT = 512
EPS = 1e-6
NCORES = 8


class Sched:
    ENG = ("pe", "act", "dve", "pool", "sp")

    def __init__(self, nc):
        self.nc = nc
        self.streams = {e: [] for e in self.ENG}
        self.cnt = {}
        self.seen = {e: {} for e in self.ENG}
        self.lastw = {}
        self.readers = {}
        self.sems = {}

    def sem(self, key):
        if key not in self.sems:
            self.sems[key] = self.nc.alloc_semaphore(name="s%d" % len(self.sems))
            self.cnt[key] = 0
        return self.sems[key]

    def op(self, eng, fn, reads=(), writes=(), dma=None, ndma=1):
        deps = {}

        def add(ev):
            if ev is None:
                return
            s, v = ev
            if deps.get(s, 0) < v:
                deps[s] = v

        for k in reads:
            add(self.lastw.get(k))
        for k in writes:
            add(self.lastw.get(k))
            for ev in self.readers.get(k, ()):
                add(ev)
        waits = []
        for s, v in deps.items():
            if s == "pe" and eng == "pe":
                continue
            if self.seen[eng].get(s, 0) >= v:
                continue
            self.seen[eng][s] = v
            waits.append((s, v))
        if dma is not None:
            self.sem(dma)
            self.cnt[dma] += 16 * ndma
            ev = (dma, self.cnt[dma])
        else:
            self.sem(eng)
            self.cnt[eng] += 1
            ev = (eng, self.cnt[eng])
        for k in reads:
            self.readers.setdefault(k, []).append(ev)
        for k in writes:
            self.lastw[k] = ev
            self.readers[k] = []
        self.streams[eng].append((waits, fn, ev, dma is not None))
        return ev

    def _replay(self, eng, e, final_waits=()):
        for waits, fn, ev, is_dma in self.streams[eng]:
            for s, v in waits:
                e.wait_ge(self.sems[s], v)
            ins = fn(e)
            if is_dma:
                for i in ins:
                    i.then_inc(self.sems[ev[0]], 16)
            else:
                ins[-1].then_inc(self.sems[ev[0]], 1)
        for s, v in final_waits:
            e.wait_ge(self.sems[s], v)

    def emit(self, block, out_sems):
        fw = [(s, self.cnt[s]) for s in out_sems]

        @block.tensor
        def _(t):
            self._replay("pe", t)

        @block.scalar
        def _(a):
            self._replay("act", a)

        @block.vector
        def _(v):
            self._replay("dve", v)

        @block.gpsimd
        def _(g):
            self._replay("pool", g)

        @block.sync
        def _(s):
            self._replay("sp", s, fw)


class Ring:
    def __init__(self, name, bufs):
        self.name = name
        self.bufs = bufs
        self.i = 0

    def next(self):
        s = self.i % len(self.bufs)
        self.i += 1
        return self.bufs[s], (self.name, s)


class Ctx:
    pass


def alloc_sb(nc, stack, name, shape, dt):
    return stack.enter_context(nc.sbuf_tensor("sb_" + name, shape, dt))


def cast_load(S, dst, dkey, src, nel_per_k, nk, semkey):
    kstep = max(1, 2048 // nel_per_k)
    pieces = [(k0, min(nk, k0 + kstep)) for k0 in range(0, nk, kstep)]

    def fn(g):
        return [g.dma_start(out=dst[:, k0:k1, :], in_=src[:, k0:k1, :]) for k0, k1 in pieces]

    S.op("pool", fn, reads=(), writes=(dkey,), dma=semkey, ndma=len(pieces))


def rmsnorm_T(S, C, xT, xkey, gcols, hT, hkey, ncol=TT, outT=None, okey=None, coff=0):
    if outT is None:
        outT, okey = hT, hkey
    half = KD // 2
    for h in range(2):
        S.op("act", lambda a, h=h: [a.activation(out=hT[:, h * half:(h + 1) * half, :ncol],
                                                 in_=xT[:, h * half:(h + 1) * half, :ncol], func=AF.Square)],
             reads=tuple((xkey, k) for k in range(h * half, (h + 1) * half)),
             writes=tuple((hkey, k) for k in range(h * half, (h + 1) * half)))
    ps, pkey = C.psum.next()

    def mm(t):
        r = []
        for k in range(KD):
            r.append(t.matmul(ps[:, :ncol], lhsT=C.ones[:, :], rhs=hT[:, k, :ncol], start=(k == 0), stop=(k == KD - 1)))
        return r

    S.op("pe", mm, reads=hkeys(hkey) + ("ones",), writes=(pkey,))
    rt, rtkey = C.tmp_ring.next()
    S.op("act", lambda a: [a.activation(out=rt[:, :ncol], in_=ps[:, :ncol], func=AF.Sqrt,
                                        bias=C.cst[:, 0:1], scale=1.0 / D)],
         reads=(pkey, "cst"), writes=(rtkey,))
    S.op("dve", lambda v: [v.reciprocal(out=C.rinv[:, :ncol], in_=rt[:, :ncol])], reads=(rtkey,), writes=("rinv",))
    for k in range(KD):
        S.op("dve", lambda v, k=k: [v.scalar_tensor_tensor(out=outT[:, k, coff:coff + ncol], in0=xT[:, k, :ncol],
                                                            scalar=gcols[:, k:k + 1], in1=C.rinv[:, :ncol],
                                                            op0=ALU.mult, op1=ALU.mult)],
             reads=((xkey, k), "rinv", (hkey, k)), writes=((okey, k),))


def hkeys(hkey):
    return tuple((hkey, k) for k in range(KD))


def xkeys(xkey):
    return tuple((xkey, k) for k in range(KD))


def proj_fm(S, C, w_dram_blk, hT, hkey, ring, semname, ncol=TT, coff=0):
    wslot, wkey = ring.next()
    cast_load(S, wslot, wkey, w_dram_blk, 128, KD, (semname, wkey[1]))
    ps, pkey = C.psum.next()

    def mm(t):
        return [t.matmul(ps[:, :ncol], lhsT=wslot[:, k, :], rhs=hT[:, k, coff:coff + ncol], start=(k == 0), stop=(k == KD - 1))
                for k in range(KD)]

    S.op("pe", mm, reads=(wkey,) + hkeys(hkey), writes=(pkey,))
    return ps, pkey


def out_proj(S, C, wo_dram, aT, akeys, xT, xkey, ncol=TT):
    for dt in range(KD):
        wslot, wkey = C.wo_ring.next()
        cast_load(S, wslot, wkey, wo_dram[dt], 128, KE, ("wo", wkey[1]))
        ps, pkey = C.psum.next()

        def mm(t, ps=ps, wslot=wslot):
            return [t.matmul(ps[:, :ncol], lhsT=wslot[:, ct, :], rhs=aT[:, ct, :ncol], start=(ct == 0), stop=(ct == KE - 1))
                    for ct in range(KE)]

        S.op("pe", mm, reads=(wkey,) + tuple(akeys), writes=(pkey,))
        S.op("dve", lambda v, ps=ps, dt=dt: [v.tensor_tensor(out=xT[:, dt, :ncol], in0=xT[:, dt, :ncol], in1=ps[:, :ncol], op=ALU.add)],
             reads=(pkey, (xkey, dt)), writes=((xkey, dt),))


def layer_A(S, C, W, xT, xkey):
    hT, hkey = C.hT, "hT"
    rmsnorm_T(S, C, xT, xkey, W["g"], hT, hkey)
    gv = C.gv
    for b in range(8):
        wslot, wkey = C.wv_ring.next()
        cast_load(S, wslot, wkey, W["wv"][b], 512, KD, ("wv", wkey[1]))
        for j in range(4):
            ps, pkey = C.psum.next()

            def mm(t, ps=ps, wslot=wslot, j=j):
                return [t.matmul(ps[:, :], lhsT=hT[:, k, j * 128:(j + 1) * 128], rhs=wslot[:, k, :], start=(k == 0), stop=(k == KD - 1))
                        for k in range(KD)]

            S.op("pe", mm, reads=(wkey,) + hkeys(hkey), writes=(pkey,))
            S.op("act", lambda a, ps=ps, j=j, b=b: [a.activation(out=gv[:, j, b * 512:(b + 1) * 512], in_=ps[:, :], func=AF.Gelu_apprx_tanh)],
                 reads=(pkey,), writes=(("gv", j, b),))
            sq, sqkey = C.tmp_ring.next()
            S.op("act", lambda a, j=j, b=b, sq=sq: [a.activation(out=sq[:, :], in_=gv[:, j, b * 512:(b + 1) * 512], func=AF.Square)],
                 reads=(("gv", j, b),), writes=(sqkey,))
            S.op("dve", lambda v, j=j, b=b, sq=sq: [v.tensor_reduce(out=C.ssp[:, j * 8 + b:j * 8 + b + 1], in_=sq[:, :], axis=AX.X, op=ALU.add)],
                 reads=(sqkey,), writes=(("ssp", j, b),))
    sspk = tuple(("ssp", j, b) for j in range(4) for b in range(8))
    S.op("dve", lambda v: [v.tensor_reduce(out=C.ss[:, 0:4], in_=C.ssp[:, :].rearrange("p (j b) -> p j b", b=8), axis=AX.X, op=ALU.add)],
         reads=sspk, writes=("ss",))
    S.op("act", lambda a: [a.activation(out=C.ss[:, 4:8], in_=C.ss[:, 0:4], func=AF.Sqrt, bias=C.cst[:, 0:1], scale=1.0 / E)],
         reads=("ss", "cst"), writes=("ss2",))
    S.op("dve", lambda v: [v.reciprocal(out=C.ss[:, 8:12], in_=C.ss[:, 4:8])], reads=("ss2",), writes=("rv",))
    for j in range(4):
        S.op("dve", lambda v, j=j: [v.tensor_scalar(out=C.Wr[:, j, :], in0=W["wsT"][:, :], scalar1=C.ss[:, 8 + j:9 + j], scalar2=None, op0=ALU.mult)],
             reads=("rv", "wsT"), writes=(("Wr", j),))
    aT = C.aT
    for ct in range(KE):
        g = ct // 4
        psu, ku = proj_fm(S, C, W["wug"][2 * ct], hT, hkey, C.wug_ring, "wug")
        gu, gukey = C.tmp_ring.next()
        S.op("act", lambda a, psu=psu, gu=gu: [a.activation(out=gu[:, :], in_=psu[:, :], func=AF.Gelu_apprx_tanh)], reads=(ku,), writes=(gukey,))
        psg, kg = proj_fm(S, C, W["wug"][2 * ct + 1], hT, hkey, C.wug_ring, "wug")
        sg, sgkey = C.tmp_ring.next()
        S.op("act", lambda a, psg=psg, sg=sg: [a.activation(out=sg[:, :], in_=psg[:, :], func=AF.Silu)], reads=(kg,), writes=(sgkey,))
        pss, ks = C.psum.next()

        def mms(t, pss=pss, ct=ct, g=g):
            return [t.matmul(pss[:, j * 128:(j + 1) * 128], lhsT=gv[:, j, ct * 128:(ct + 1) * 128], rhs=C.Wr[:, j, g * 128:(g + 1) * 128],
                             start=True, stop=True) for j in range(4)]

        S.op("pe", mms, reads=tuple(("gv", j, ct // 4) for j in range(4)) + tuple(("Wr", j) for j in range(4)), writes=(ks,))
        sv, svkey = C.tmp_ring.next()
        S.op("dve", lambda v, pss=pss, sv=sv, ct=ct, g=g: [v.scalar_tensor_tensor(out=sv[:, j * 128:(j + 1) * 128], in0=pss[:, j * 128:(j + 1) * 128],
                                                                               scalar=W["vgain"][:, ct:ct + 1], in1=W["bsb"][:, g, :],
                                                                               op0=ALU.mult, op1=ALU.add) for j in range(4)],
             reads=(ks,), writes=(svkey,))
        S.op("dve", lambda v, sv=sv, gu=gu: [v.tensor_tensor(out=sv[:, :], in0=sv[:, :], in1=gu[:, :], op=ALU.mult)],
             reads=(svkey, gukey), writes=(svkey,))
        S.op("dve", lambda v, sv=sv, sg=sg, ct=ct: [v.tensor_tensor(out=aT[:, ct, :], in0=sv[:, :], in1=sg[:, :], op=ALU.mult)],
             reads=(svkey, sgkey), writes=(("aT", ct),))
    out_proj(S, C, W["wo"], aT, [("aT", ct) for ct in range(KE)], xT, xkey)


def to_T(x2d):
    t = x2d.shape[0]
    return np.ascontiguousarray(x2d.T.reshape(KD, 128, t).transpose(1, 0, 2))


def from_T(xT):
    t = xT.shape[2]
    return np.ascontiguousarray(xT.transpose(1, 0, 2).reshape(D, t).T)


def cols_pk(vec, nk):
    return np.ascontiguousarray(vec.reshape(nk, 128).T)


def w_blocks_fm(w, col0, nblk, nk):
    sub = w[:, col0:col0 + nblk * 128].reshape(nk, 128, nblk, 128)
    return np.ascontiguousarray(sub.transpose(2, 1, 0, 3))


def w_blocks_wide(w, col0, nblk, nk, width):
    sub = w[:, col0:col0 + nblk * width].reshape(nk, 128, nblk, width)
    return np.ascontiguousarray(sub.transpose(2, 1, 0, 3))


def prep_A(a_norm, a_w_in, a_v_gain, a_w_s, a_b_s, a_w_out, j):
    a_norm, a_w_in, a_v_gain, a_w_s, a_b_s, a_w_out = [np.asarray(v, np.float32) for v in (a_norm, a_w_in, a_v_gain, a_w_s, a_b_s, a_w_out)]
    w_in = a_w_in[j]
    u = w_blocks_fm(w_in, 0, KE, KD)
    gt = w_blocks_fm(w_in, 2 * E, KE, KD)
    wug = np.empty((2 * KE, 128, KD, 128), np.float32)
    wug[0::2] = u
    wug[1::2] = gt
    wv = w_blocks_wide(w_in, E, 8, KD, 512)
    wo = w_blocks_fm(a_w_out[j], 0, KD, KE)
    wsT = np.ascontiguousarray(a_w_s[j].transpose(2, 0, 1).reshape(128, 8 * 128))
    bsb = np.ascontiguousarray(np.broadcast_to(a_b_s[j][None], (128, 8, 128)))
    return {"g": cols_pk(a_norm[j], KD), "wug": wug, "wv": wv, "wo": wo, "wsT": wsT, "bsb": bsb,
            "vgain": cols_pk(a_v_gain[j], KE)}


A_SHAPES = {"g": [128, KD], "wug": [2 * KE, 128, KD, 128], "wv": [8, 128, KD, 512], "wo": [KD, 128, KE, 128],
            "wsT": [128, 1024], "bsb": [128, 8, 128], "vgain": [128, KE]}


def const_inputs():
    cst = np.zeros((128, 8), np.float32)
    cst[:, 0] = EPS
    return {"cst": cst, "ones_f": np.ones((128, 128), np.float32)}


def declare_A(nc, pre):
    return {k: nc.dram_tensor(pre + k, shp, F32, kind="ExternalInput").ap() for k, shp in A_SHAPES.items()}


def load_small_A(S, C, stack, nc, Wd, pre):
    W = dict(Wd)
    for k in ("g", "wsT", "bsb", "vgain"):
        t = alloc_sb(nc, stack, pre + k + "_sb", A_SHAPES[k], F32)
        S.op("sp", lambda s, t=t, k=k: [s.dma_start(out=t[:], in_=Wd[k][:])], writes=(pre + k,), dma=pre + k)
        W[k] = t
    return W


def common_ctx(nc, stack, S, cst_d, ones_d):
    C = Ctx()
    C.psum = Ring("ps", [stack.enter_context(nc.psum_tensor("ps%d" % i, [128, 512], F32)) for i in range(8)])
    C.ones = alloc_sb(nc, stack, "ones", [128, 128], BF16)
    C.cst = alloc_sb(nc, stack, "cst", [128, 8], F32)
    C.rinv = alloc_sb(nc, stack, "rinv", [128, TT], F32)
    S.op("pool", lambda g: [g.dma_start(out=C.ones[:], in_=ones_d[:])], writes=("ones",), dma="ones")
    S.op("sp", lambda s: [s.dma_start(out=C.cst[:], in_=cst_d[:])], writes=("cst",), dma="cstl")
    return C


def ctx_A(nc, stack, C):
    C.hT = alloc_sb(nc, stack, "hT", [128, KD, TT], BF16)
    C.gv = alloc_sb(nc, stack, "gv", [128, 4, E], BF16)
    C.aT = alloc_sb(nc, stack, "aT", [128, KE, TT], BF16)
    C.ssp = alloc_sb(nc, stack, "ssp", [128, 32], F32)
    C.ss = alloc_sb(nc, stack, "ss", [128, 12], F32)
    C.Wr = alloc_sb(nc, stack, "Wr", [128, 4, 1024], BF16)
    C.wv_ring = Ring("wvs", [alloc_sb(nc, stack, "wvs%d" % i, [128, KD, 512], BF16) for i in range(2)])
    C.wug_ring = Ring("wugs", [alloc_sb(nc, stack, "wugs%d" % i, [128, KD, 128], BF16) for i in range(3)])
    C.wo_ring = Ring("wos", [alloc_sb(nc, stack, "wos%d" % i, [128, KE, 128], BF16) for i in range(2)])
    C.tmp_ring = Ring("tmp", [alloc_sb(nc, stack, "tmp%d" % i, [128, TT], F32) for i in range(6)])


def build_L1(ntile=4):
    from contextlib import ExitStack
    nc = bass.Bass("TRN2", target_bir_lowering=False)
    ntok = ntile * TT
    xT_d = nc.dram_tensor("xT", [128, KD, ntok], F32, kind="ExternalInput").ap()
    cst_d = nc.dram_tensor("cst", [128, 8], F32, kind="ExternalInput").ap()
    ones_d = nc.dram_tensor("ones_f", [128, 128], F32, kind="ExternalInput").ap()
    gB_d = nc.dram_tensor("gB", [128, KD], F32, kind="ExternalInput").ap()
    Wd = declare_A(nc, "a0_")
    x1_d = nc.dram_tensor("x1T", [128, KD, ntok], F32, kind="ExternalOutput").ap()
    hB_d = nc.dram_tensor("hBT", [128, KD, ntok], BF16, kind="ExternalOutput").ap()
    S = Sched(nc)
    with ExitStack() as stack:
        C = common_ctx(nc, stack, S, cst_d, ones_d)
        ctx_A(nc, stack, C)
        W = load_small_A(S, C, stack, nc, Wd, "a0_")
        gB = alloc_sb(nc, stack, "gB_sb", [128, KD], F32)
        S.op("sp", lambda s: [s.dma_start(out=gB[:], in_=gB_d[:])], writes=("gcolsB",), dma="gBl")
        xT = alloc_sb(nc, stack, "xTsb", [128, KD, TT], F32)
        for i in range(ntile):
            c0 = i * TT
            S.op("sp", lambda s, c0=c0: [s.dma_start(out=xT[:], in_=xT_d[:, :, c0:c0 + TT])], writes=xkeys("xT"), dma="xload")
            layer_A(S, C, W, xT, "xT")
            S.op("sp", lambda s, c0=c0: [s.dma_start(out=x1_d[:, :, c0:c0 + TT], in_=xT[:])], reads=xkeys("xT"), writes=("x1_d",), dma="x1st")
            rmsnorm_T(S, C, xT, "xT", gB, C.hT, "hT")
            S.op("sp", lambda s, c0=c0: [s.dma_start(out=hB_d[:, :, c0:c0 + TT], in_=C.hT[:])], reads=hkeys("hT"), writes=("hB_d",), dma="hBst")
        with nc.Block() as block:
            S.emit(block, ["x1st", "hBst"])
    return nc


POOLW = (2, 4, 8, 16)
HW = TT + 16


def layer_C(S, C, W, xT, xkey, hTe, hekey, left_edge, right_edge):
    aT = C.aT
    for gi, w in enumerate(POOLW):
        hp, hpkey = C.hp_ring.next()
        for k in range(KD):
            he = hTe[:, k, :]
            cur, curkey = C.tmp_ring.next()
            S.op("dve", lambda v, cur=cur, he=he: [v.tensor_tensor(out=cur[:, 1:527], in0=he[:, 0:526], in1=he[:, 1:527], op=ALU.add)],
                 reads=((hekey, k),), writes=(curkey,))
            for st, (lo, hi, sh) in enumerate(((2, 526, 1), (4, 524, 2), (8, 520, 4))):
                if gi <= st:
                    break
                nxt, nxtkey = C.tmp_ring.next()
                S.op("dve", lambda v, cur=cur, nxt=nxt, lo=lo, hi=hi, sh=sh: [v.tensor_tensor(out=nxt[:, lo:hi], in0=cur[:, lo - sh:hi - sh],
                                                                                             in1=cur[:, lo + sh:hi + sh], op=ALU.add)],
                     reads=(curkey,), writes=(nxtkey,))
                cur, curkey = nxt, nxtkey
            S.op("dve", lambda v, cur=cur, he=he, hp=hp, k=k, w=w: [v.scalar_tensor_tensor(out=hp[:, k, :], in0=cur[:, 8:520], scalar=1.0 / w,
                                                                                      in1=he[:, 8:520], op0=ALU.mult, op1=ALU.subtract)],
                 reads=(curkey, (hekey, k)), writes=((hpkey, k),))
            for edge, (c0, e0) in ((left_edge, (0, 0)), (right_edge, (504, 8))):
                if not edge:
                    continue
                et, etkey = C.tmp_ring.next()
                S.op("dve", lambda v, cur=cur, et=et, c0=c0, e0=e0, gi=gi: [v.tensor_tensor(out=et[:, 0:8], in0=cur[:, 8 + c0:16 + c0],
                                                                                           in1=W["icnt"][:, gi, e0:e0 + 8], op=ALU.mult)],
                     reads=(curkey,), writes=(etkey,))
                S.op("dve", lambda v, et=et, he=he, hp=hp, k=k, c0=c0: [v.tensor_tensor(out=hp[:, k, c0:c0 + 8], in0=et[:, 0:8],
                                                                                        in1=he[:, 8 + c0:16 + c0], op=ALU.subtract)],
                     reads=(etkey, (hekey, k), (hpkey, k)), writes=((hpkey, k),))
        pT, pTkey = C.pT_ring.next()
        for cc in range(8):
            ct = gi * 8 + cc
            ps, pk = proj_fm(S, C, W["wxc"][ct], hp, hpkey, C.wug_ring, "wug")
            S.op("act", lambda a, ps=ps, pT=pT, cc=cc: [a.activation(out=pT[:, cc, :], in_=ps[:, :], func=AF.Copy)],
                 reads=(pk,), writes=((pTkey, cc),))
        for dd in range(8):
            ct = gi * 8 + dd
            psg, kg = proj_fm(S, C, W["wgt"][ct], hTe, hekey, C.wug_ring, "wug", coff=8)
            sg, sgkey = C.tmp_ring.next()
            S.op("act", lambda a, psg=psg, sg=sg: [a.activation(out=sg[:, :TT], in_=psg[:, :], func=AF.Silu)], reads=(kg,), writes=(sgkey,))
            wm, wmkey = C.wmix_ring.next()
            cast_load(S, wm, wmkey, W["wmix"][ct], 128, 8, ("wmix", wmkey[1]))
            psy, ky = C.psum.next()

            def mmy(t, psy=psy, wm=wm, pT=pT):
                return [t.matmul(psy[:, :], lhsT=wm[:, cc, :], rhs=pT[:, cc, :], start=(cc == 0), stop=(cc == 7)) for cc in range(8)]

            S.op("pe", mmy, reads=(wmkey,) + tuple((pTkey, cc) for cc in range(8)), writes=(ky,))
            S.op("dve", lambda v, psy=psy, sg=sg, ct=ct: [v.scalar_tensor_tensor(out=aT[:, ct, :], in0=psy[:, :], scalar=W["cscale"][:, ct:ct + 1],
                                                                               in1=sg[:, :TT], op0=ALU.mult, op1=ALU.mult)],
                 reads=(ky, sgkey), writes=(("aT", ct),))
    out_proj(S, C, W["wo"], aT, [("aT", ct) for ct in range(KE)], xT, xkey)


def prep_C(c_norm, c_w_in, c_w_mix, c_scale, c_w_out):
    c_norm, c_w_in, c_w_mix, c_scale, c_w_out = [np.asarray(v, np.float32) for v in (c_norm, c_w_in, c_w_mix, c_scale, c_w_out)]
    w_in = c_w_in[0]
    wxc = w_blocks_fm(w_in, 0, KE, KD)
    wgt = w_blocks_fm(w_in, E, KE, KD)
    wmix = np.empty((KE, 128, 8, 128), np.float32)
    for gi in range(4):
        wmix[gi * 8:(gi + 1) * 8] = w_blocks_fm(c_w_mix[0, gi], 0, 8, 8)
    wo = w_blocks_fm(c_w_out[0], 0, KD, KE)
    return {"g": cols_pk(c_norm[0], KD), "wxc": wxc, "wgt": wgt, "wmix": wmix, "wo": wo, "cscale": cols_pk(c_scale[0], KE)}


def icnt_table(hh, seq=4096, half=2048):
    t = np.zeros((4, 16), np.float32)
    for gi, w in enumerate(POOLW):
        for i in range(8):
            for side in range(2):
                own = i if side == 0 else half - 8 + i
                tg = hh * half + own
                lo = max(tg - w // 2, 0)
                hi = min(tg + w - 1 - w // 2, seq - 1)
                t[gi, side * 8 + i] = 1.0 / (hi - lo + 1)
    return np.ascontiguousarray(np.broadcast_to(t[None], (128, 4, 16)))


C_SHAPES = {"g": [128, KD], "wxc": [KE, 128, KD, 128], "wgt": [KE, 128, KD, 128], "wmix": [KE, 128, 8, 128],
            "wo": [KD, 128, KE, 128], "cscale": [128, KE], "icnt": [128, 4, 16]}


def build_L3(ntile=4):
    from contextlib import ExitStack
    nc = bass.Bass("TRN2", target_bir_lowering=False)
    ntok = ntile * TT
    x1_d = nc.dram_tensor("x1T", [128, KD, ntok], F32, kind="ExternalInput").ap()
    x1h_d = nc.dram_tensor("x1h", [128, KD, 16], F32, kind="ExternalInput").ap()
    aB_d = nc.dram_tensor("aBT", [128, KE, ntok], BF16, kind="ExternalInput").ap()
    aBh_d = nc.dram_tensor("aBh", [128, KE, 16], BF16, kind="ExternalInput").ap()
    cst_d = nc.dram_tensor("cst", [128, 8], F32, kind="ExternalInput").ap()
    ones_d = nc.dram_tensor("ones_f", [128, 128], F32, kind="ExternalInput").ap()
    woB_d = nc.dram_tensor("woB", [KD, 128, KE, 128], F32, kind="ExternalInput").ap()
    Wd = {k: nc.dram_tensor("c_" + k, shp, F32, kind="ExternalInput").ap() for k, shp in C_SHAPES.items()}
    x2_d = nc.dram_tensor("x2s", [128, KD, ntok], F32, kind="Internal").ap()
    hCe_d = nc.dram_tensor("hCe", [128, KD, ntok + 16], BF16, kind="Internal").ap()
    x3_d = nc.dram_tensor("x3T", [128, KD, ntok], F32, kind="ExternalOutput").ap()
    S = Sched(nc)
    with ExitStack() as stack:
        C = common_ctx(nc, stack, S, cst_d, ones_d)
        C.tmp_ring = Ring("tmp", [alloc_sb(nc, stack, "tmp%d" % i, [128, HW], F32) for i in range(8)])
        C.aT = alloc_sb(nc, stack, "aT", [128, KE, TT], BF16)
        C.hTe = alloc_sb(nc, stack, "hTe", [128, KD, HW], BF16)
        C.hp_ring = Ring("hp", [alloc_sb(nc, stack, "hp%d" % i, [128, KD, TT], BF16) for i in range(2)])
        C.pT_ring = Ring("pT", [alloc_sb(nc, stack, "pT%d" % i, [128, 8, TT], BF16) for i in range(2)])
        C.wug_ring = Ring("wugs", [alloc_sb(nc, stack, "wugs%d" % i, [128, KD, 128], BF16) for i in range(4)])
        C.wo_ring = Ring("wos", [alloc_sb(nc, stack, "wos%d" % i, [128, KE, 128], BF16) for i in range(2)])
        C.wmix_ring = Ring("wmx", [alloc_sb(nc, stack, "wmx%d" % i, [128, 8, 128], BF16) for i in range(3)])
        W = dict(Wd)
        for k in ("g", "cscale", "icnt"):
            t = alloc_sb(nc, stack, "c_" + k, C_SHAPES[k], F32)
            S.op("sp", lambda s, t=t, k=k: [s.dma_start(out=t[:], in_=Wd[k][:])], writes=("c_" + k,), dma="c_" + k)
            W[k] = t
        xT = alloc_sb(nc, stack, "xTsb", [128, KD, TT], F32)
        aT = C.aT
        akeys = [("aT", ct) for ct in range(KE)]
        for i in range(ntile):
            c0 = i * TT
            S.op("sp", lambda s, c0=c0: [s.dma_start(out=xT[:], in_=x1_d[:, :, c0:c0 + TT])], writes=xkeys("xT"), dma="xload")
            S.op("sp", lambda s, c0=c0: [s.dma_start(out=aT[:], in_=aB_d[:, :, c0:c0 + TT])], writes=akeys, dma="aload")
            out_proj(S, C, woB_d, aT, akeys, xT, "xT")
            S.op("sp", lambda s, c0=c0: [s.dma_start(out=x2_d[:, :, c0:c0 + TT], in_=xT[:])], reads=xkeys("xT"), writes=(("x2_d", i),), dma="x2st")
            rmsnorm_T(S, C, xT, "xT", W["g"], C.hTe, "hTe")
            S.op("sp", lambda s, c0=c0: [s.dma_start(out=hCe_d[:, :, 8 + c0:8 + c0 + TT], in_=C.hTe[:, :, 0:TT])], reads=hkeys("hTe"),
                 writes=(("hCe_d", i),), dma="hCst")
        S.op("sp", lambda s: [s.dma_start(out=xT[:, :, 0:16], in_=x1h_d[:])], writes=xkeys("xT"), dma="xload")
        S.op("sp", lambda s: [s.dma_start(out=aT[:, :, 0:16], in_=aBh_d[:])], writes=akeys, dma="aload")
        out_proj(S, C, woB_d, aT, akeys, xT, "xT", ncol=16)
        rmsnorm_T(S, C, xT, "xT", W["g"], C.hTe, "hTe", ncol=16)
        S.op("sp", lambda s: [s.dma_start(out=hCe_d[:, :, 0:8], in_=C.hTe[:, :, 0:8]),
                              s.dma_start(out=hCe_d[:, :, 8 + ntok:16 + ntok], in_=C.hTe[:, :, 8:16])], reads=hkeys("hTe"),
             writes=(("hCe_d", "halo"),), dma="hCst", ndma=2)
        hall = tuple(("hCe_d", i) for i in range(ntile)) + (("hCe_d", "halo"),)
        for i in range(ntile):
            c0 = i * TT
            S.op("sp", lambda s, c0=c0: [s.dma_start(out=xT[:], in_=x2_d[:, :, c0:c0 + TT])], reads=(("x2_d", i),), writes=xkeys("xT"), dma="xload")
            S.op("sp", lambda s, c0=c0: [s.dma_start(out=C.hTe[:], in_=hCe_d[:, :, c0:c0 + HW])], reads=hall, writes=hkeys("hTe"), dma="hload")
            layer_C(S, C, W, xT, "xT", C.hTe, "hTe", i == 0, i == ntile - 1)
            S.op("sp", lambda s, c0=c0: [s.dma_start(out=x3_d[:, :, c0:c0 + TT], in_=xT[:])], reads=xkeys("xT"), writes=("x3_d",), dma="x3st")
        with nc.Block() as block:
            S.emit(block, ["x3st"])
    return nc


def build_L4(ntile=4):
    from contextlib import ExitStack
    nc = bass.Bass("TRN2", target_bir_lowering=False)
    ntok = ntile * TT
    xT_d = nc.dram_tensor("xT", [128, KD, ntok], F32, kind="ExternalInput").ap()
    cst_d = nc.dram_tensor("cst", [128, 8], F32, kind="ExternalInput").ap()
    ones_d = nc.dram_tensor("ones_f", [128, 128], F32, kind="ExternalInput").ap()
    gF_d = nc.dram_tensor("gF", [128, KD], F32, kind="ExternalInput").ap()
    Wd = declare_A(nc, "a1_")
    o_d = nc.dram_tensor("oT", [128, KD, ntok], F32, kind="ExternalOutput").ap()
    S = Sched(nc)
    with ExitStack() as stack:
        C = common_ctx(nc, stack, S, cst_d, ones_d)
        ctx_A(nc, stack, C)
        W = load_small_A(S, C, stack, nc, Wd, "a1_")
        gF = alloc_sb(nc, stack, "gF_sb", [128, KD], F32)
        S.op("sp", lambda s: [s.dma_start(out=gF[:], in_=gF_d[:])], writes=("gcolsF",), dma="gFl")
        xT = alloc_sb(nc, stack, "xTsb", [128, KD, TT], F32)
        for i in range(ntile):
            c0 = i * TT
            S.op("sp", lambda s, c0=c0: [s.dma_start(out=xT[:], in_=xT_d[:, :, c0:c0 + TT])], writes=xkeys("xT"), dma="xload")
            layer_A(S, C, W, xT, "xT")
            rmsnorm_T(S, C, xT, "xT", gF, C.hT, "hT", outT=xT, okey="xT")
            S.op("sp", lambda s, c0=c0: [s.dma_start(out=o_d[:, :, c0:c0 + TT], in_=xT[:])], reads=xkeys("xT"), writes=("o_d",), dma="ost")
        with nc.Block() as block:
            S.emit(block, ["ost"])
    return nc


SEQ = 4096
NST = SEQ // TT
HT2 = 256


def dft_consts():
    bf = ml_dtypes.bfloat16
    n = np.arange(SEQ, dtype=np.int64)
    prod = (n[:, None] * n[None, :]) % SEQ
    ang = prod.astype(np.float64) * (2.0 * np.pi / SEQ)
    Cs = np.cos(ang)
    Ss = np.sin(ang)

    def lay(m):
        return np.ascontiguousarray(m.reshape(32, 128, NST, TT).transpose(2, 1, 0, 3).astype(np.float32)).astype(bf)

    c = np.arange(512, dtype=np.int64)
    a2 = ((c[:, None] * c[None, :]) % 512).astype(np.float64) * (2.0 * np.pi / 512)
    sc = 1.0 / np.sqrt(float(SEQ) * 512.0)
    C5 = (np.cos(a2) * sc).astype(np.float32).reshape(4, 128, 512).transpose(1, 0, 2)
    S5 = (-np.sin(a2) * sc).astype(np.float32).reshape(4, 128, 512).transpose(1, 0, 2)
    return {"dftC": lay(Cs), "dftS": lay(Ss), "c512": np.ascontiguousarray(C5).astype(bf), "s512n": np.ascontiguousarray(S5).astype(bf)}


def prep_B(b_w_in, b_w_mix, hh):
    b_w_in, b_w_mix = np.asarray(b_w_in, np.float32), np.asarray(b_w_mix, np.float32)
    w_in = b_w_in[0]
    wxb = w_blocks_wide(w_in, hh * 2048, 4, KD, 512)
    wgt = w_blocks_fm(w_in, E + hh * 2048, 16, KD)
    wmix = np.stack([w_blocks_wide(b_w_mix[0, hh * 4 + gl], 0, 1, 4, 512)[0] for gl in range(4)])
    return {"wxb": wxb, "wgt": wgt, "wmixB": wmix}


def build_L2():
    from contextlib import ExitStack
    nc = bass.Bass("TRN2", target_bir_lowering=False)
    hB_d = nc.dram_tensor("hBT", [128, KD, SEQ], BF16, kind="ExternalInput").ap()
    dC_d = nc.dram_tensor("dftC", [NST, 128, 32, TT], BF16, kind="ExternalInput").ap()
    dS_d = nc.dram_tensor("dftS", [NST, 128, 32, TT], BF16, kind="ExternalInput").ap()
    c5_d = nc.dram_tensor("c512", [128, 4, 512], BF16, kind="ExternalInput").ap()
    s5_d = nc.dram_tensor("s512n", [128, 4, 512], BF16, kind="ExternalInput").ap()
    wxb_d = nc.dram_tensor("wxb", [4, 128, KD, 512], F32, kind="ExternalInput").ap()
    wgt_d = nc.dram_tensor("wgt", [16, 128, KD, 128], F32, kind="ExternalInput").ap()
    wmx_d = nc.dram_tensor("wmixB", [4, 128, 4, 512], F32, kind="ExternalInput").ap()
    aB_d = nc.dram_tensor("aBT", [128, 16, SEQ], BF16, kind="ExternalOutput").ap()
    S = Sched(nc)
    with ExitStack() as stack:
        C = Ctx()
        C.psum = Ring("ps", [stack.enter_context(nc.psum_tensor("ps%d" % i, [128, 512], F32)) for i in range(8)])
        X = alloc_sb(nc, stack, "X", [128, 32, 512], BF16)
        sgT = alloc_sb(nc, stack, "sgT", [128, 4, SEQ], BF16)
        hring = Ring("hB", [alloc_sb(nc, stack, "hB%d" % i, [128, KD, HT2], BF16) for i in range(2)])
        dCr = Ring("dC", [alloc_sb(nc, stack, "dC%d" % i, [128, 8, TT], BF16) for i in range(2)])
        dSr = Ring("dS", [alloc_sb(nc, stack, "dS%d" % i, [128, 8, TT], BF16) for i in range(2)])
        pqr = Ring("pq", [alloc_sb(nc, stack, "pq%d" % i, [128, 8, TT], BF16) for i in range(2)])
        aor = Ring("ao", [alloc_sb(nc, stack, "ao%d" % i, [128, 4, TT], BF16) for i in range(2)])
        wxb = alloc_sb(nc, stack, "wxbs", [128, KD, 512], BF16)
        wg = alloc_sb(nc, stack, "wgs", [128, 4 * KD, 128], BF16)
        wmx = alloc_sb(nc, stack, "wmxs", [128, 4, 512], BF16)
        Wcs = alloc_sb(nc, stack, "Wcs", [128, 8, 512], BF16)
        c5 = alloc_sb(nc, stack, "c5s", [128, 4, 512], BF16)
        s5 = alloc_sb(nc, stack, "s5s", [128, 4, 512], BF16)
        S.op("sp", lambda s: [s.dma_start(out=c5[:], in_=c5_d[:])], writes=("c5",), dma="c5l")
        S.op("sp", lambda s: [s.dma_start(out=s5[:], in_=s5_d[:])], writes=("s5",), dma="s5l")
        for gl in range(4):
            cast_load(S, wxb, "wxb", wxb_d[gl], 512, KD, "wxbl")
            for dd in range(4):
                cast_load(S, wg[:, dd * KD:(dd + 1) * KD, :], ("wg", dd), wgt_d[gl * 4 + dd], 128, KD, ("wgl", dd))
            cast_load(S, wmx, "wmx", wmx_d[gl], 512, 4, "wmxl")
            for mi, (cm, cmk) in enumerate(((c5, "c5"), (s5, "s5"))):
                for cc in range(4):
                    ps, pk = C.psum.next()
                    S.op("pe", lambda t, ps=ps, cm=cm, cc=cc: [t.matmul(ps[:, :], lhsT=cm[:, k, cc * 128:(cc + 1) * 128], rhs=wmx[:, k, :],
                                                                    start=(k == 0), stop=(k == 3)) for k in range(4)],
                         reads=(cmk, "wmx"), writes=(pk,))
                    S.op("dve", lambda v, ps=ps, mi=mi, cc=cc: [v.tensor_copy(out=Wcs[:, mi * 4 + cc, :], in_=ps[:, :])],
                         reads=(pk,), writes=(("Wcs", mi * 4 + cc),))
            for ht in range(SEQ // HT2):
                hb, hbk = hring.next()
                S.op("sp", lambda s, hb=hb, ht=ht: [s.dma_start(out=hb[:], in_=hB_d[:, :, ht * HT2:(ht + 1) * HT2])], writes=(hbk,), dma=("hBl", hbk[1]))
                for j in range(HT2 // 128):
                    tt = ht * (HT2 // 128) + j
                    ps, pk = C.psum.next()
                    S.op("pe", lambda t, ps=ps, hb=hb, j=j: [t.matmul(ps[:, :], lhsT=hb[:, k, j * 128:(j + 1) * 128], rhs=wxb[:, k, :],
                                                                  start=(k == 0), stop=(k == KD - 1)) for k in range(KD)],
                         reads=(hbk, "wxb"), writes=(pk,))
                    if tt % 2 == 0:
                        S.op("dve", lambda v, ps=ps, tt=tt: [v.tensor_copy(out=X[:, tt, :], in_=ps[:, :])], reads=(pk,), writes=(("X", tt),))
                    else:
                        S.op("act", lambda a, ps=ps, tt=tt: [a.activation(out=X[:, tt, :], in_=ps[:, :], func=AF.Copy)], reads=(pk,), writes=(("X", tt),))
                for dd in range(4):
                    ps, pk = C.psum.next()
                    S.op("pe", lambda t, ps=ps, hb=hb, dd=dd: [t.matmul(ps[:, :HT2], lhsT=wg[:, dd * KD + k, :], rhs=hb[:, k, :],
                                                                    start=(k == 0), stop=(k == KD - 1)) for k in range(KD)],
                         reads=(hbk, ("wg", dd)), writes=(pk,))
                    S.op("act", lambda a, ps=ps, dd=dd, ht=ht: [a.activation(out=sgT[:, dd, ht * HT2:(ht + 1) * HT2], in_=ps[:, :HT2], func=AF.Silu)],
                         reads=(pk,), writes=(("sgT", dd, ht // 2),))
            xk = tuple(("X", tt) for tt in range(32))
            for st in range(NST):
                banks = [C.psum.next() for _ in range(8)]
                for q in range(4):
                    dc, dck = dCr.next()
                    ds, dsk = dSr.next()
                    S.op("sp", lambda s, dc=dc, st=st, q=q: [s.dma_start(out=dc[:], in_=dC_d[st, :, q * 8:(q + 1) * 8, :])], writes=(dck,), dma=("dCl", dck[1]))
                    S.op("sp", lambda s, ds=ds, st=st, q=q: [s.dma_start(out=ds[:], in_=dS_d[st, :, q * 8:(q + 1) * 8, :])], writes=(dsk,), dma=("dSl", dsk[1]))
                    for cc in range(4):
                        for mi, (dm, dmk) in enumerate(((dc, dck), (ds, dsk))):
                            ps, pk = banks[mi * 4 + cc]
                            S.op("pe", lambda t, ps=ps, dm=dm, cc=cc, q=q: [t.matmul(ps[:, :], lhsT=X[:, q * 8 + kk, cc * 128:(cc + 1) * 128], rhs=dm[:, kk, :],
                                                                                 start=(q == 0 and kk == 0), stop=(q == 3 and kk == 7)) for kk in range(8)],
                                 reads=(dmk,) + xk, writes=(pk,))
                pq, pqk = pqr.next()
                for bi in range(8):
                    ps, pk = banks[bi]
                    if bi % 2 == 0:
                        S.op("dve", lambda v, ps=ps, pq=pq, bi=bi: [v.tensor_copy(out=pq[:, bi, :], in_=ps[:, :])], reads=(pk,), writes=((pqk, bi),))
                    else:
                        S.op("act", lambda a, ps=ps, pq=pq, bi=bi: [a.activation(out=pq[:, bi, :], in_=ps[:, :], func=AF.Copy)], reads=(pk,), writes=((pqk, bi),))
                ao, aok = aor.next()
                for dd in range(4):
                    ps, pk = C.psum.next()
                    S.op("pe", lambda t, ps=ps, pq=pq, dd=dd: [t.matmul(ps[:, :], lhsT=Wcs[:, bi, dd * 128:(dd + 1) * 128], rhs=pq[:, bi, :],
                                                                    start=(bi == 0), stop=(bi == 7)) for bi in range(8)],
                         reads=tuple((pqk, bi) for bi in range(8)) + tuple(("Wcs", bi) for bi in range(8)), writes=(pk,))
                    S.op("dve", lambda v, ps=ps, ao=ao, dd=dd, st=st: [v.tensor_tensor(out=ao[:, dd, :], in0=ps[:, :], in1=sgT[:, dd, st * TT:(st + 1) * TT], op=ALU.mult)],
                         reads=(pk, ("sgT", dd, st)), writes=((aok, dd),))
                S.op("sp", lambda s, ao=ao, gl=gl, st=st: [s.dma_start(out=aB_d[:, gl * 4:(gl + 1) * 4, st * TT:(st + 1) * TT], in_=ao[:])],
                     reads=tuple((aok, dd) for dd in range(4)), writes=("aB_d",), dma="aBst")
        with nc.Block() as block:
            S.emit(block, ["aBst"])
    return nc


def run_L1(xT_list, Wh, gB, ntile=4):
    nc = build_L1(ntile)
    cst = const_inputs()
    in_maps = []
    for xT in xT_list:
        m = {"xT": xT, "gB": gB, **cst}
        for k, v in Wh.items():
            m["a0_" + k] = v
        in_maps.append(m)
    res = run_bass_kernel_spmd(nc, in_maps, core_ids=list(range(len(in_maps))))
    return [(r["x1T"], r["hBT"]) for r in res.results]


def _run(nc, in_maps):
    res = run_bass_kernel_spmd(nc, in_maps, core_ids=list(range(len(in_maps))))
    return res.results


def kernel(**inputs):
    x = np.asarray(inputs["x"], np.float32)
    xf = x.reshape(-1, D)
    cst = const_inputs()
    half = 2048
    WA0 = prep_A(inputs["a_norm"], inputs["a_w_in"], inputs["a_v_gain"], inputs["a_w_s"], inputs["a_b_s"], inputs["a_w_out"], 0)
    gB = cols_pk(np.asarray(inputs["b_norm"])[0], KD)
    in1 = []
    for c in range(NCORES):
        m = {"xT": to_T(xf[c * half:(c + 1) * half]), "gB": gB, **cst}
        m.update({"a0_" + k: v for k, v in WA0.items()})
        in1.append(m)
    r1 = _run(build_L1(), in1)
    x1T = [r["x1T"] for r in r1]
    hBT = [r["hBT"] for r in r1]
    del in1, WA0
    dft = dft_consts()
    WB = [prep_B(inputs["b_w_in"], inputs["b_w_mix"], hh) for hh in range(2)]
    in2 = []
    for c in range(NCORES):
        b, hh = c // 2, c % 2
        m = {"hBT": np.ascontiguousarray(np.concatenate([hBT[2 * b], hBT[2 * b + 1]], axis=2)), **dft, **WB[hh]}
        in2.append(m)
    r2 = _run(build_L2(), in2)
    aBl = [r["aBT"] for r in r2]
    del in2, WB, dft
    WC = prep_C(inputs["c_norm"], inputs["c_w_in"], inputs["c_w_mix"], inputs["c_scale"], inputs["c_w_out"])
    woB = w_blocks_fm(np.asarray(inputs["b_w_out"])[0], 0, KD, KE)
    in3 = []
    for c in range(NCORES):
        b, t = c // 2, c % 2
        aB = np.concatenate([aBl[2 * b][:, :, t * half:(t + 1) * half], aBl[2 * b + 1][:, :, t * half:(t + 1) * half]], axis=1)
        x1h = np.zeros((128, KD, 16), np.float32)
        aBh = np.zeros((128, KE, 16), aB.dtype)
        if t == 1:
            x1h[:, :, 0:8] = x1T[c - 1][:, :, half - 8:half]
            aBh[:, :, 0:8] = np.concatenate([aBl[2 * b][:, :, half - 8:half], aBl[2 * b + 1][:, :, half - 8:half]], axis=1)
        else:
            x1h[:, :, 8:16] = x1T[c + 1][:, :, 0:8]
            aBh[:, :, 8:16] = np.concatenate([aBl[2 * b][:, :, half:half + 8], aBl[2 * b + 1][:, :, half:half + 8]], axis=1)
        m = {"x1T": x1T[c], "x1h": x1h, "aBT": np.ascontiguousarray(aB), "aBh": aBh, "woB": woB, **cst}
        m.update({"c_" + k: v for k, v in WC.items()})
        m["c_icnt"] = icnt_table(t)
        in3.append(m)
    r3 = _run(build_L3(), in3)
    x3T = [r["x3T"] for r in r3]
    del in3, WC, woB
    WA1 = prep_A(inputs["a_norm"], inputs["a_w_in"], inputs["a_v_gain"], inputs["a_w_s"], inputs["a_b_s"], inputs["a_w_out"], 1)
    gF = cols_pk(np.asarray(inputs["final_norm"]), KD)
    in4 = []
    for c in range(NCORES):
        m = {"xT": x3T[c], "gF": gF, **cst}
        m.update({"a1_" + k: v for k, v in WA1.items()})
        in4.append(m)
    r4 = _run(build_L4(), in4)
    out = np.empty((NCORES * half, D), np.float32)
    for c in range(NCORES):
        out[c * half:(c + 1) * half] = from_T(r4[c]["oT"])
    return out.reshape(x.shape)
```
